# Optimizing a Trainium2 kernel written in Bass

```python
import math
import jax, jax.numpy as jnp
from jax import lax
import numpy as np

D_MODEL = 1024
BATCH = 4
SEQ = 8192
DEPTH = 2

HEAD_DIM = 64
ROT_DIM = HEAD_DIM // 4
ROPE_THETA = 500000.0
NORM_EPS = 1e-6
D_FF = 2816

MOBA_HEADS = 8
MOBA_BLOCK = 256
MOBA_TOPK = 3
MOBA_Q_CHUNK = 64
MOBA_WIDTH = MOBA_HEADS * HEAD_DIM
GMLP_GROUPS = 8
GMLP_GROUP_DIM = 64
GMLP_CHUNK = 128
GMLP_WIDTH = GMLP_GROUPS * GMLP_GROUP_DIM
EVEN_IN = 3 * MOBA_WIDTH + 2 * GMLP_WIDTH
EVEN_OUT = MOBA_WIDTH + GMLP_WIDTH

DIFF_HEADS = 8
DIFF_QK_DIM = HEAD_DIM
DIFF_V_DIM = 2 * HEAD_DIM
DIFF_QK_WIDTH = DIFF_HEADS * 2 * DIFF_QK_DIM
DIFF_WIDTH = DIFF_HEADS * DIFF_V_DIM
ODD_IN = 2 * DIFF_QK_WIDTH + DIFF_WIDTH
DIFF_Q_BLOCK = 128

N_EVEN = (DEPTH + 1) // 2
N_ODD = DEPTH // 2

kernel_name = 'hybrid_moba_gmlp_diffattn_macaron'


def rms_norm(x, g):
    xf = x.astype(jnp.float32)
    y = xf * lax.rsqrt(jnp.mean(xf * xf, axis=-1, keepdims=True) + NORM_EPS)
    return (y * g.astype(jnp.float32)).astype(x.dtype)


def layer_norm(x, g, b):
    xf = x.astype(jnp.float32)
    mu = jnp.mean(xf, axis=-1, keepdims=True)
    var = jnp.mean(jnp.square(xf - mu), axis=-1, keepdims=True)
    y = (xf - mu) * lax.rsqrt(var + NORM_EPS)
    return (y * g.astype(jnp.float32) + b.astype(jnp.float32)).astype(x.dtype)


def swiglu(x, w_gate, w_up, w_down):
    return (jax.nn.silu(x @ w_gate) * (x @ w_up)) @ w_down


def rope_tables(seq_len):
    inv = 1.0 / (ROPE_THETA ** (jnp.arange(0, ROT_DIM, 2, dtype=jnp.float32) / ROT_DIM))
    ang = jnp.arange(seq_len, dtype=jnp.float32)[:, None] * inv[None, :]
    return jnp.cos(ang), jnp.sin(ang)


def apply_partial_rope(x, cos, sin):
    half = ROT_DIM // 2
    x1, x2, rest = x[..., :half], x[..., half:ROT_DIM], x[..., ROT_DIM:]
    c, s = cos.astype(x.dtype), sin.astype(x.dtype)
    return jnp.concatenate([x1 * c - x2 * s, x1 * s + x2 * c, rest], axis=-1)


_gather_blocks = jax.vmap(jax.vmap(lambda tab, idx: tab[idx]))


def moba_attention(q, k, v):
    B, H, S, Dh = q.shape
    s_pad = -(-S // MOBA_BLOCK) * MOBA_BLOCK
    pad = ((0, 0), (0, 0), (0, s_pad - S), (0, 0))
    q, k, v = jnp.pad(q, pad), jnp.pad(k, pad), jnp.pad(v, pad)
    n_blocks = s_pad // MOBA_BLOCK
    kb = k.reshape(B, H, n_blocks, MOBA_BLOCK, Dh)
    vb = v.reshape(B, H, n_blocks, MOBA_BLOCK, Dh)
    k_mean = jnp.mean(kb.astype(jnp.float32), axis=3).astype(q.dtype)
    scale = Dh ** -0.5
    n_top = min(MOBA_TOPK, n_blocks - 1)
    block_ids = jnp.arange(n_blocks)

    def chunk(c):
        start = c * MOBA_Q_CHUNK
        blk = start // MOBA_BLOCK
        qc = lax.dynamic_slice_in_dim(q, start, MOBA_Q_CHUNK, axis=2)
        q_pos = start + jnp.arange(MOBA_Q_CHUNK)
        k_pos = blk * MOBA_BLOCK + jnp.arange(MOBA_BLOCK)
        k_own = lax.dynamic_index_in_dim(kb, blk, axis=2, keepdims=False)
        v_own = lax.dynamic_index_in_dim(vb, blk, axis=2, keepdims=False)
        s_own = jnp.einsum('bhqd,bhkd->bhqk', qc, k_own).astype(jnp.float32) * scale
        s_own = jnp.where(k_pos[None, :] <= q_pos[:, None], s_own, -jnp.inf)
        if n_top == 0:
            p_own = jax.nn.softmax(s_own, axis=-1).astype(v.dtype)
            return jnp.einsum('bhqk,bhkd->bhqd', p_own, v_own)
        gate = jnp.einsum('bhqd,bhnd->bhqn', qc, k_mean).astype(jnp.float32)
        gate = jnp.where(block_ids < blk, gate, -jnp.inf)
        _, idx = lax.top_k(gate, n_top)
        valid = idx < blk
        k_sel = _gather_blocks(kb, idx)
        v_sel = _gather_blocks(vb, idx)
        s_sel = jnp.einsum('bhqd,bhqnkd->bhqnk', qc, k_sel).astype(jnp.float32) * scale
        s_sel = jnp.where(valid[..., None], s_sel, -jnp.inf)
        scores = jnp.concatenate(
            [s_sel.reshape(B, H, MOBA_Q_CHUNK, n_top * MOBA_BLOCK), s_own], axis=-1)
        p = jax.nn.softmax(scores, axis=-1).astype(v.dtype)
        p_sel = p[..., :n_top * MOBA_BLOCK].reshape(B, H, MOBA_Q_CHUNK, n_top, MOBA_BLOCK)
        p_own = p[..., n_top * MOBA_BLOCK:]
        return (jnp.einsum('bhqnk,bhqnkd->bhqd', p_sel, v_sel)
                + jnp.einsum('bhqk,bhkd->bhqd', p_own, v_own))

    n_chunks = s_pad // MOBA_Q_CHUNK
    out = lax.map(chunk, jnp.arange(n_chunks))
    out = out.transpose(1, 2, 0, 3, 4).reshape(B, H, s_pad, Dh)
    return out[:, :, :S]


def chunked_spatial_gating(z, ln_g, ln_b, w_s, b_s):
    B, S, _ = z.shape
    u, v = z[..., :GMLP_WIDTH], z[..., GMLP_WIDTH:]
    v = layer_norm(v.reshape(B, S, GMLP_GROUPS, GMLP_GROUP_DIM), ln_g, ln_b)
    v = v.reshape(B, S // GMLP_CHUNK, GMLP_CHUNK, GMLP_GROUPS, GMLP_GROUP_DIM)
    causal = jnp.tril(jnp.ones((GMLP_CHUNK, GMLP_CHUNK), dtype=bool))
    w = jnp.where(causal[None], w_s, jnp.zeros_like(w_s))
    mixed = jnp.einsum('gts,bnsgd->bntgd', w, v) + b_s.T[None, None, :, :, None]
    return u * mixed.reshape(B, S, GMLP_WIDTH)


def diff_attention(q, k, v, lam, subln_g, lambda_init):
    B, H, _, S, Dqk = q.shape
    scale = Dqk ** -0.5
    k_pos = jnp.arange(S)

    def block(i):
        start = i * DIFF_Q_BLOCK
        qb = lax.dynamic_slice_in_dim(q, start, DIFF_Q_BLOCK, axis=3)
        s = jnp.einsum('bhcqd,bhckd->bhcqk', qb, k).astype(jnp.float32) * scale
        q_pos = start + jnp.arange(DIFF_Q_BLOCK)
        s = jnp.where(k_pos[None, :] <= q_pos[:, None], s, -jnp.inf)
        p = jax.nn.softmax(s, axis=-1)
        a = (p[:, :, 0] - lam * p[:, :, 1]).astype(v.dtype)
        o = jnp.einsum('bhqk,bhkd->bhqd', a, v)
        return rms_norm(o, subln_g) * (1.0 - lambda_init)

    out = lax.map(block, jnp.arange(S // DIFF_Q_BLOCK))
    return out.transpose(1, 0, 3, 2, 4).reshape(B, S, H * v.shape[-1])


def even_mixer(h, cos, sin, w_in, w_out, ln_g, ln_b, w_s, b_s):
    B, S, _ = h.shape
    z = h @ w_in
    q = z[..., :MOBA_WIDTH]
    k = z[..., MOBA_WIDTH:2 * MOBA_WIDTH]
    v = z[..., 2 * MOBA_WIDTH:3 * MOBA_WIDTH]
    gz = z[..., 3 * MOBA_WIDTH:]
    to_heads = lambda t: t.reshape(B, S, MOBA_HEADS, HEAD_DIM).transpose(0, 2, 1, 3)
    q = apply_partial_rope(to_heads(q), cos, sin)
    k = apply_partial_rope(to_heads(k), cos, sin)
    attn = moba_attention(q, k, to_heads(v))
    attn = attn.transpose(0, 2, 1, 3).reshape(B, S, MOBA_WIDTH)
    gated = chunked_spatial_gating(jax.nn.gelu(gz), ln_g, ln_b, w_s, b_s)
    return jnp.concatenate([attn, gated], axis=-1) @ w_out


def odd_mixer(h, cos, sin, w_in, w_out, lq1, lk1, lq2, lk2, subln_g, lambda_init):
    B, S, _ = h.shape
    z = h @ w_in
    to_qk = lambda t: t.reshape(B, S, DIFF_HEADS, 2, DIFF_QK_DIM).transpose(0, 2, 3, 1, 4)
    q = apply_partial_rope(to_qk(z[..., :DIFF_QK_WIDTH]), cos, sin)
    k = apply_partial_rope(to_qk(z[..., DIFF_QK_WIDTH:2 * DIFF_QK_WIDTH]), cos, sin)
    v = z[..., 2 * DIFF_QK_WIDTH:].reshape(B, S, DIFF_HEADS, DIFF_V_DIM).transpose(0, 2, 1, 3)
    f32 = jnp.float32
    lam = (jnp.exp(jnp.sum(lq1.astype(f32) * lk1.astype(f32)))
           - jnp.exp(jnp.sum(lq2.astype(f32) * lk2.astype(f32))) + lambda_init)
    return diff_attention(q, k, v, lam, subln_g, lambda_init) @ w_out


def setup_inputs(seed: int = 0) -> dict:
    key = jax.random.key(seed)
    ks = iter(jax.random.split(key, 32))
    nrm = lambda shape, scale: jax.random.normal(next(ks), shape, jnp.float32) * scale
    gain = lambda shape: 1.0 + nrm(shape, 0.02)
    return {
        'x': nrm((BATCH, SEQ, D_MODEL), 1.0),
        'ffn_pre_norm': gain((DEPTH, D_MODEL)),
        'ffn_pre_w_gate': nrm((DEPTH, D_MODEL, D_FF), D_MODEL ** -0.5),
        'ffn_pre_w_up': nrm((DEPTH, D_MODEL, D_FF), D_MODEL ** -0.5),
        'ffn_pre_w_down': nrm((DEPTH, D_FF, D_MODEL), D_FF ** -0.5),
        'mix_norm': gain((DEPTH, D_MODEL)),
        'ffn_post_norm': gain((DEPTH, D_MODEL)),
        'ffn_post_w_gate': nrm((DEPTH, D_MODEL, D_FF), D_MODEL ** -0.5),
        'ffn_post_w_up': nrm((DEPTH, D_MODEL, D_FF), D_MODEL ** -0.5),
        'ffn_post_w_down': nrm((DEPTH, D_FF, D_MODEL), D_FF ** -0.5),
        'even_w_in': nrm((N_EVEN, D_MODEL, EVEN_IN), D_MODEL ** -0.5),
        'even_w_out': nrm((N_EVEN, EVEN_OUT, D_MODEL), EVEN_OUT ** -0.5),
        'gmlp_ln_g': gain((N_EVEN, GMLP_GROUPS, GMLP_GROUP_DIM)),
        'gmlp_ln_b': nrm((N_EVEN, GMLP_GROUPS, GMLP_GROUP_DIM), 0.02),
        'gmlp_w_s': nrm((N_EVEN, GMLP_GROUPS, GMLP_CHUNK, GMLP_CHUNK), GMLP_CHUNK ** -0.5),
        'gmlp_b_s': 1.0 + nrm((N_EVEN, GMLP_GROUPS, GMLP_CHUNK), 0.02),
        'odd_w_in': nrm((N_ODD, D_MODEL, ODD_IN), D_MODEL ** -0.5),
        'odd_w_out': nrm((N_ODD, DIFF_WIDTH, D_MODEL), DIFF_WIDTH ** -0.5),
        'diff_lambda_q1': nrm((N_ODD, DIFF_QK_DIM), 0.1),
        'diff_lambda_k1': nrm((N_ODD, DIFF_QK_DIM), 0.1),
        'diff_lambda_q2': nrm((N_ODD, DIFF_QK_DIM), 0.1),
        'diff_lambda_k2': nrm((N_ODD, DIFF_QK_DIM), 0.1),
        'diff_subln_g': gain((N_ODD, DIFF_V_DIM)),
        'final_norm': gain((D_MODEL,)),
    }


def reference(x, ffn_pre_norm, ffn_pre_w_gate, ffn_pre_w_up, ffn_pre_w_down, mix_norm,
              ffn_post_norm, ffn_post_w_gate, ffn_post_w_up, ffn_post_w_down,
              even_w_in, even_w_out, gmlp_ln_g, gmlp_ln_b, gmlp_w_s, gmlp_b_s,
              odd_w_in, odd_w_out, diff_lambda_q1, diff_lambda_k1, diff_lambda_q2,
              diff_lambda_k2, diff_subln_g, final_norm):
    cos, sin = rope_tables(x.shape[1])
    for layer in range(DEPTH):
        x = x + 0.5 * swiglu(rms_norm(x, ffn_pre_norm[layer]), ffn_pre_w_gate[layer],
                             ffn_pre_w_up[layer], ffn_pre_w_down[layer])
        h = rms_norm(x, mix_norm[layer])
        if layer % 2 == 0:
            e = layer // 2
            x = x + even_mixer(h, cos, sin, even_w_in[e], even_w_out[e], gmlp_ln_g[e],
                               gmlp_ln_b[e], gmlp_w_s[e], gmlp_b_s[e])
        else:
            o = layer // 2
            lambda_init = 0.8 - 0.6 * math.exp(-0.3 * layer)
            x = x + odd_mixer(h, cos, sin, odd_w_in[o], odd_w_out[o], diff_lambda_q1[o],
                              diff_lambda_k1[o], diff_lambda_q2[o], diff_lambda_k2[o],
                              diff_subln_g[o], lambda_init)
        x = x + 0.5 * swiglu(rms_norm(x, ffn_post_norm[layer]), ffn_post_w_gate[layer],
                             ffn_post_w_up[layer], ffn_post_w_down[layer])
    return rms_norm(x, final_norm)
```

```python
import math
import numpy as np
import concourse.bass as bass
import concourse.mybir as mybir
from concourse.bass_utils import run_bass_kernel_spmd

F32 = mybir.dt.float32
BF16 = mybir.dt.bfloat16
AF = mybir.ActivationFunctionType
ALU = mybir.AluOpType
AX = mybir.AxisListType

D_MODEL = 1024
SEQ = 8192
BATCH = 4
D_FF = 2816
NCH = D_MODEL // 128
NFF = D_FF // 128
EPS = 1e-6
TT = 512
NEG = -30000.0


class Clock:
    def __init__(self, sem):
        self.sem = sem
        self.count = 0


class View:
    def __init__(self, buf, ap):
        self.buf = buf
        self.ap = ap

    def __getitem__(self, idx):
        return View(self.buf, self.ap[idx])


class Buf:
    def __init__(self, ap, name=""):
        self.ap = ap
        self.name = name
        self.w = None
        self.r = {}
        self.dclk = None
        self.p = None

    def split(self):
        self.p = [Buf(self.ap[:, c], f"{self.name}.{c}") for c in range(self.ap.shape[1])]
        return self

    def leaves(self):
        return self.p if self.p else [self]

    def __getitem__(self, idx):
        return View(self, self.ap[idx])

    @property
    def v(self):
        return View(self, self.ap)


class Eng:
    def __init__(self, name, h, clk):
        self.name = name
        self.h = h
        self.clk = clk
        self.seen = {}


class Ctx:
    def __init__(self, nc):
        self.nc = nc
        self.nsem = 0
        self.pe = Eng("pe", nc.tensor, self.clock("pe"))
        self.act = Eng("act", nc.scalar, self.clock("act"))
        self.dve = Eng("dve", nc.vector, self.clock("dve"))
        self.pool = Eng("pool", nc.gpsimd, self.clock("pool"))
        self.sp = Eng("sp", nc.sync, self.clock("sp"))
        self.engs = [self.pe, self.act, self.dve, self.pool, self.sp]
        self.n_instr = 0

    def clock(self, name):
        self.nsem += 1
        return Clock(self.nc.alloc_semaphore(f"{name}_{self.nsem}"))

    def sbuf(self, name, shape, dtype):
        return Buf(self.nc.alloc_sbuf_tensor(name, list(shape), dtype)[:], name)

    def psum(self, name, shape, dtype=F32):
        return Buf(self.nc.alloc_psum_tensor(name, list(shape), dtype)[:], name)

    def dram(self, name, shape, dtype, kind="Internal"):
        return Buf(self.nc.dram_tensor(name, list(shape), dtype, kind=kind).ap(), name)

    def sub(self, view, name=""):
        return Buf(view.ap, name)

    def _sync(self, eng, rb, wb):
        deps = {}

        def add(st, raw=False):
            if st is None:
                return
            clk, val = st
            if clk is eng.clk:
                if eng is self.pe or val > clk.count:
                    return
            if deps.get(clk, 0) < val:
                deps[clk] = val

        for b in rb:
            add(b.w, raw=True)
        for b in wb:
            add(b.w)
            for clk, val in b.r.items():
                add((clk, val))
        for clk, val in deps.items():
            if eng.seen.get(clk, 0) >= val:
                continue
            eng.h.wait_ge(clk.sem, val)
            eng.seen[clk] = val

    @staticmethod
    def _stamp(st, rb, wb):
        clk, val = st
        for b in rb:
            if b.r.get(clk, 0) < val:
                b.r[clk] = val
        for b in wb:
            b.w = st
            b.r = {}

    def op(self, eng, fn, reads, writes, signal=True):
        rb = [l for v in reads for l in v.buf.leaves()]
        wb = [l for v in writes for l in v.buf.leaves()]
        self._sync(eng, rb, wb)
        ins = fn()
        self.n_instr += 1
        if signal:
            eng.clk.count += 1
            ins.then_inc(eng.clk.sem, 1)
            st = (eng.clk, eng.clk.count)
        else:
            st = (eng.clk, eng.clk.count + 1)
        self._stamp(st, rb, wb)
        return ins

    def dma(self, q, out, in_, clkbuf=None):
        rb = in_.buf.leaves()
        wb = out.buf.leaves()
        if out.ap.dtype != in_.ap.dtype:
            q = self.pool
        self._sync(q, rb, wb)
        b = clkbuf if clkbuf is not None else out.buf
        if b.dclk is None:
            b.dclk = self.clock("d")
        ins = q.h.dma_start(out=out.ap, in_=in_.ap)
        self.n_instr += 1
        b.dclk.count += 16
        ins.then_inc(b.dclk.sem, 16)
        self._stamp((b.dclk, b.dclk.count), rb, wb)
        return ins

    def wait_all(self, eng, bufs):
        self._sync(eng, [], [l for b in bufs for l in b.leaves()])

    def mm(self, out, lhsT, rhs, start, stop, extra_reads=(), signal=None):
        nc = self.nc
        return self.op(self.pe,
                       lambda: nc.tensor.matmul(out.ap, lhsT=lhsT.ap, rhs=rhs.ap, start=start, stop=stop),
                       [lhsT, rhs] + list(extra_reads), [out], signal=(stop if signal is None else signal))

    def activation(self, out, in_, func, bias=None, scale=1.0, eng=None):
        nc = self.nc
        reads = [in_]
        kw = {}
        if bias is not None:
            if isinstance(bias, View):
                reads.append(bias)
                kw["bias"] = bias.ap
            else:
                kw["bias"] = bias
        if isinstance(scale, View):
            reads.append(scale)
            kw["scale"] = scale.ap
        else:
            kw["scale"] = scale
        return self.op(self.act, lambda: nc.scalar.activation(out=out.ap, in_=in_.ap, func=func, **kw),
                       reads, [out])

    def _veng(self, eng):
        return self.dve if eng is None else eng

    def tt(self, out, in0, in1, op, eng=None):
        e = self._veng(eng)
        return self.op(e, lambda: e.h.tensor_tensor(out=out.ap, in0=in0.ap, in1=in1.ap, op=op),
                       [in0, in1], [out])

    def ts(self, out, in0, s1, s2, op0, op1=None, eng=None):
        e = self._veng(eng)
        reads = [in0]
        a1 = s1
        a2 = s2
        if isinstance(s1, View):
            reads.append(s1)
            a1 = s1.ap
        if isinstance(s2, View):
            reads.append(s2)
            a2 = s2.ap
        if op1 is None:
            return self.op(e, lambda: e.h.tensor_scalar(out=out.ap, in0=in0.ap, scalar1=a1, scalar2=None, op0=op0),
                           reads, [out])
        return self.op(e, lambda: e.h.tensor_scalar(out=out.ap, in0=in0.ap, scalar1=a1, scalar2=a2, op0=op0, op1=op1),
                       reads, [out])

    def stt(self, out, in0, scalar, in1, op0, op1, eng=None):
        e = self._veng(eng)
        reads = [in0, in1]
        sc = scalar
        if isinstance(scalar, View):
            reads.append(scalar)
            sc = scalar.ap
        return self.op(e, lambda: e.h.scalar_tensor_tensor(out=out.ap, in0=in0.ap, scalar=sc, in1=in1.ap,
                                                           op0=op0, op1=op1), reads, [out])

    def copy(self, out, in_, eng=None):
        e = self._veng(eng)
        return self.op(e, lambda: e.h.tensor_copy(out=out.ap, in_=in_.ap), [in_], [out])

    def memset(self, out, val, eng=None):
        e = self._veng(eng)
        return self.op(e, lambda: e.h.memset(out.ap, val), [], [out])

    def recip(self, out, in_):
        nc = self.nc
        return self.op(self.dve, lambda: nc.vector.reciprocal(out=out.ap, in_=in_.ap), [in_], [out])

    def finish(self, out_bufs):
        for e in self.engs:
            pass
        self.wait_all(self.sp, out_bufs)


class Common:
    def __init__(self, cx, nbanks=8):
        self.cx = cx
        nc = cx.nc
        self.banks = [cx.psum(f"bank{i}", [128, TT], F32) for i in range(nbanks)]
        self.ones_bf = cx.sbuf("ones_bf", [128, 128], BF16)
        cx.memset(self.ones_bf.v, 1.0)
        self.eps_t = cx.sbuf("eps_t", [128, 1], F32)
        cx.memset(self.eps_t.v, EPS)


def emit_rmsnorm(cx, cm, xt, gcol, hT, stat_bank, sq, rstd, post_scale=None, out_f32=None):
    for c in range(NCH):
        s = sq[c % len(sq)]
        cx.activation(s.v, xt.p[c].v, AF.Square)
        cx.mm(stat_bank.v, cm.ones_bf.v, s.v, start=(c == 0), stop=(c == NCH - 1), signal=True)
    cx.activation(rstd.v, stat_bank.v, AF.Sqrt, bias=cm.eps_t.v, scale=1.0 / D_MODEL)
    cx.recip(rstd.v, rstd.v)
    for c in range(NCH):
        dst = hT.p[c].v if out_f32 is None else out_f32.p[c].v
        cx.stt(dst, xt.p[c].v, gcol[:, c:c + 1], rstd.v, ALU.mult, ALU.mult)


class FFNWeights:
    def __init__(self, cx, name):
        self.wg = cx.dram(f"{name}_wg", [NFF, 128, NCH * 128], F32, kind="ExternalInput")
        self.wu = cx.dram(f"{name}_wu", [NFF, 128, NCH * 128], F32, kind="ExternalInput")
        self.wd = cx.dram(f"{name}_wd", [NCH, 128, NFF * 128], F32, kind="ExternalInput")


def host_ffn_weights(name, wg, wu, wd):
    def t_in(w):
        return np.ascontiguousarray(w.reshape(NCH, 128, NFF, 128).transpose(2, 1, 0, 3).reshape(NFF, 128, NCH * 128))
    wdt = np.ascontiguousarray(wd.reshape(NFF, 128, NCH, 128).transpose(2, 1, 0, 3).reshape(NCH, 128, NFF * 128))
    return {f"{name}_wg": t_in(wg), f"{name}_wu": t_in(wu), f"{name}_wd": wdt}


class FFNScratch:
    def __init__(self, cx, S):
        self.S = S
        self.hT = [cx.sbuf(f"hT{s}", [128, NCH, TT], BF16).split() for s in range(S)]
        self.aT = [[cx.sbuf(f"aT{s}_{f}", [128, TT], BF16) for f in range(NFF)] for s in range(S)]
        self.wg = [cx.sbuf(f"wg{i}", [128, NCH * 128], BF16) for i in range(2)]
        self.wu = [cx.sbuf(f"wu{i}", [128, NCH * 128], BF16) for i in range(2)]
        self.wd = [cx.sbuf(f"wd{i}", [128, NFF * 128], BF16) for i in range(2)]
        self.sg = [cx.sbuf(f"sg{i}", [128, TT], F32) for i in range(2)]
        self.sq = [cx.sbuf(f"sq{i}", [128, TT], BF16) for i in range(2)]
        self.rstd = cx.sbuf("rstd", [128, TT], F32)


def emit_ffn(cx, cm, xts, gcol, W, sc, wq):
    S = len(xts)
    B = cm.banks
    for s in range(S):
        emit_rmsnorm(cx, cm, xts[s], gcol, sc.hT[s], B[6], sc.sq, sc.rstd)
    def load_gu(f):
        cx.dma(wq, sc.wg[f % 2].v, W.wg[f])
        cx.dma(wq, sc.wu[f % 2].v, W.wu[f])
    load_gu(0)
    it = 0
    for f in range(NFF):
        if f + 1 < NFF:
            load_gu(f + 1)
        wg = sc.wg[f % 2]
        wu = sc.wu[f % 2]
        for s in range(S):
            bg = B[(it % 2) * 2]
            bu = B[(it % 2) * 2 + 1]
            sg = sc.sg[it % 2]
            it += 1
            for c in range(NCH):
                cx.mm(bg.v, wg[:, c * 128:(c + 1) * 128], sc.hT[s].p[c].v, start=(c == 0), stop=(c == NCH - 1))
            for c in range(NCH):
                cx.mm(bu.v, wu[:, c * 128:(c + 1) * 128], sc.hT[s].p[c].v, start=(c == 0), stop=(c == NCH - 1))
            cx.activation(sg.v, bg.v, AF.Silu)
            cx.tt(sc.aT[s][f].v, bu.v, sg.v, ALU.mult)
    def load_d(j):
        cx.dma(wq, sc.wd[j % 2].v, W.wd[j])
    load_d(0)
    it = 0
    for j in range(NCH):
        if j + 1 < NCH:
            load_d(j + 1)
        wd = sc.wd[j % 2]
        for s in range(S):
            by = B[4 + (it % 2)]
            it += 1
            for f in range(NFF):
                cx.mm(by.v, wd[:, f * 128:(f + 1) * 128], sc.aT[s][f].v, start=(f == 0), stop=(f == NFF - 1))
            cx.stt(xts[s].p[j].v, by.v, 0.5, xts[s].p[j].v, ALU.mult, ALU.add)


def build_rowlocal(ntok, n_ffn, has_outproj, final, act_dt=BF16, S=2):
    nc = bass.Bass("TRN2", target_bir_lowering=False)
    cx = Ctx(nc)
    cm = Common(cx)
    xT = cx.dram("xT", [D_MODEL, ntok], F32, kind="ExternalInput")
    gains = cx.dram("gains", [128, (n_ffn + 1) * NCH], F32, kind="ExternalInput")
    Ws = [FFNWeights(cx, f"ffn{k}") for k in range(n_ffn)]
    if has_outproj:
        mixT = cx.dram("mixT", [D_MODEL, ntok], act_dt, kind="ExternalInput")
        wout = cx.dram("wout", [128, NCH * D_MODEL], F32, kind="ExternalInput")
    if final:
        outT = cx.dram("outT", [D_MODEL, ntok], F32, kind="ExternalOutput")
        outs = [outT]
    else:
        xTo = cx.dram("xTo", [D_MODEL, ntok], F32, kind="ExternalOutput")
        hTo = cx.dram("hTo", [D_MODEL, ntok], act_dt, kind="ExternalOutput")
        outs = [xTo, hTo]
    emit_rowlocal(cx, cm, ntok, xT, gains, Ws, mixT if has_outproj else None, wout if has_outproj else None,
                  final, outs, S)
    cx.finish(outs)
    return nc


def dram_tile(d, t0, nt):
    return View(d, d.ap.rearrange("(c p) t -> p c t", p=128)[:, :, t0:t0 + nt])


def emit_rowlocal(cx, cm, ntok, xT, gains, Ws, mixT, wout, final, outs, S=2):
    n_ffn = len(Ws)
    sc = FFNScratch(cx, S)
    g_sb = cx.sbuf("g_sb", [128, (n_ffn + 1) * NCH], F32)
    cx.dma(cx.sp, g_sb.v, gains.v)
    xts = [cx.sbuf(f"xt{s}", [128, NCH, TT], F32).split() for s in range(S)]
    if mixT is not None:
        wout_sb = cx.sbuf("wout_sb", [128, NCH * D_MODEL], BF16)
        cx.dma(cx.pool, wout_sb.v, wout.v)
        mts = [cx.sbuf(f"mt{s}", [128, NCH, TT], BF16).split() for s in range(S)]
    stage = [cx.sbuf(f"ostage{i}", [128, TT], F32) for i in range(2)]
    nsup = ntok // (S * TT)
    B = cm.banks
    for u in range(nsup):
        for s in range(S):
            t0 = (u * S + s) * TT
            cx.dma(cx.sp, xts[s].v, dram_tile(xT, t0, TT))
            if mixT is not None:
                cx.dma(cx.sp, mts[s].v, dram_tile(mixT, t0, TT))
        if mixT is not None:
            it = 0
            for s in range(S):
                for j in range(NCH):
                    by = B[4 + (it % 2)]
                    it += 1
                    for c in range(NCH):
                        cx.mm(by.v, wout_sb[:, c * D_MODEL + j * 128: c * D_MODEL + (j + 1) * 128], mts[s].p[c].v,
                              start=(c == 0), stop=(c == NCH - 1))
                    cx.tt(xts[s].p[j].v, by.v, xts[s].p[j].v, ALU.add)
        for k in range(n_ffn):
            emit_ffn(cx, cm, xts, g_sb[:, k * NCH:(k + 1) * NCH], Ws[k], sc, cx.pool)
        gl = g_sb[:, n_ffn * NCH:(n_ffn + 1) * NCH]
        for s in range(S):
            t0 = (u * S + s) * TT
            if final:
                emit_rmsnorm_stats(cx, cm, xts[s], B[6], sc.sq, sc.rstd)
                for c in range(NCH):
                    st = stage[c % 2]
                    cx.stt(st.v, xts[s].p[c].v, gl[:, c:c + 1], sc.rstd.v, ALU.mult, ALU.mult)
                    cx.dma(cx.sp, View(outs[0], outs[0].ap[c * 128:(c + 1) * 128, t0:t0 + TT]), st.v, clkbuf=st)
            else:
                emit_rmsnorm(cx, cm, xts[s], gl, sc.hT[s], B[6], sc.sq, sc.rstd)
                cx.dma(cx.sp, dram_tile(outs[0], t0, TT), xts[s].v, clkbuf=xts[s])
                cx.dma(cx.sp, dram_tile(outs[1], t0, TT), sc.hT[s].v, clkbuf=sc.hT[s])


def emit_rmsnorm_stats(cx, cm, xt, stat_bank, sq, rstd):
    for c in range(NCH):
        s = sq[c % len(sq)]
        cx.activation(s.v, xt.p[c].v, AF.Square)
        cx.mm(stat_bank.v, cm.ones_bf.v, s.v, start=(c == 0), stop=(c == NCH - 1), signal=True)
    cx.activation(rstd.v, stat_bank.v, AF.Sqrt, bias=cm.eps_t.v, scale=1.0 / D_MODEL)
    cx.recip(rstd.v, rstd.v)


HD = 64
ROT = 16
LAMBDA_INIT_1 = 0.8 - 0.6 * math.exp(-0.3 * 1)


def rope_tables_host(ns):
    inv = 1.0 / (500000.0 ** (np.arange(0, ROT, 2, dtype=np.float32) / ROT))
    ang = np.arange(ns, dtype=np.float32)[:, None] * inv[None, :]
    cos, sin = np.cos(ang).astype(np.float32), np.sin(ang).astype(np.float32)
    ct = np.ones((128, ns), np.float32)
    st = np.zeros((128, ns), np.float32)
    for base in (0, 64):
        ct[base:base + 8] = cos.T
        ct[base + 8:base + 16] = cos.T
        st[base:base + 8] = -sin.T
        st[base + 8:base + 16] = sin.T
    return ct, st


def swap_cols(w):
    w4 = w.reshape(w.shape[0], -1, 64)
    out = w4.copy()
    out[:, :, 0:8] = w4[:, :, 8:16]
    out[:, :, 8:16] = w4[:, :, 0:8]
    return out.reshape(w.shape)


def wtile(w):
    n = w.shape[1] // 128
    return np.ascontiguousarray(w.reshape(NCH, 128, n, 128).transpose(2, 1, 0, 3).reshape(n, 128, NCH * 128))


def tri_host():
    t = np.zeros((128, 4, TT), np.float32)
    k = np.arange(128)[:, None]
    q = np.arange(TT)[None, :]
    for j in range(4):
        t[:, j, :] = np.where(q >= 128 * j + k, 0.0, NEG)
    return np.ascontiguousarray(t.reshape(128, 4 * TT))


def emit_rope_proj(cx, cm, hTt, w, ws, cos, sin, dst, banks, tmp):
    b0, b1 = banks
    for c in range(NCH):
        cx.mm(b0.v, w[:, c * 128:(c + 1) * 128], hTt.p[c].v, start=(c == 0), stop=(c == NCH - 1))
    for c in range(NCH):
        cx.mm(b1.v, ws[:, c * 128:(c + 1) * 128], hTt.p[c].v, start=(c == 0), stop=(c == NCH - 1))
    t1, t2 = tmp
    cx.tt(t1.v, b0.v, cos.v, ALU.mult)
    cx.tt(t2.v, b1.v, sin.v, ALU.mult)
    cx.tt(dst, t1.v, t2.v, ALU.add)


def build_mixer1(ns, nh, act_dt=BF16):
    nc = bass.Bass("TRN2", target_bir_lowering=False)
    cx = Ctx(nc)
    cm = Common(cx)
    hT = cx.dram("hT", [D_MODEL, ns], act_dt, kind="ExternalInput")
    wts = {n: cx.dram(n, [nh, 128, NCH * 128], F32, kind="ExternalInput") for n in ("wq", "wqs", "wk", "wks", "wv")}
    cosT = cx.dram("cosT", [128, ns], F32, kind="ExternalInput")
    sinT = cx.dram("sinT", [128, ns], F32, kind="ExternalInput")
    tri = cx.dram("tri", [128, 4 * TT], F32, kind="ExternalInput")
    ident = cx.dram("ident", [128, 128], F32, kind="ExternalInput")
    lam = cx.dram("lam", [128, 4 * HD], F32, kind="ExternalInput")
    gsub = cx.dram("gsub", [128, 1], F32, kind="ExternalInput")
    mixo = cx.dram("mixo", [nh * 128, ns], act_dt, kind="ExternalOutput")
    emit_mixer1(cx, cm, ns, nh, hT, wts, cosT, sinT, tri, ident, lam, gsub, mixo)
    cx.finish([mixo])
    return nc


def emit_mixer1(cx, cm, ns, nh, hT, wts, cosT, sinT, tri, ident, lam, gsub, mixo):
    B = cm.banks
    ntile = ns // TT
    nkt = ns // 128
    scale = HD ** -0.5
    tri_sb = cx.sbuf("tri_sb", [128, 4 * TT], BF16)
    cx.dma(cx.pool, tri_sb.v, tri.v)
    id_sb = cx.sbuf("id_sb", [128, 128], BF16)
    cx.dma(cx.pool, id_sb.v, ident.v)
    lam_sb = cx.sbuf("lam_sb", [128, 4 * HD], F32)
    cx.dma(cx.sp, lam_sb.v, lam.v)
    gs_sb = cx.sbuf("gs_sb", [128, 1], F32)
    cx.dma(cx.sp, gs_sb.v, gsub.v)
    lp = cx.sbuf("lam_p", [128, 2 * HD], F32)
    cx.tt(lp[:, 0:HD], lam_sb[:, 0:HD], lam_sb[:, HD:2 * HD], ALU.mult)
    cx.tt(lp[:, HD:2 * HD], lam_sb[:, 2 * HD:3 * HD], lam_sb[:, 3 * HD:4 * HD], ALU.mult)
    l2 = cx.sbuf("lam_2", [128, 2], F32)
    nc = cx.nc
    cx.op(cx.dve, lambda: nc.vector.reduce_sum(out=l2.ap, in_=lp.ap.rearrange("p (a d) -> p a d", a=2), axis=AX.X),
          [lp.v], [l2.v])
    cx.activation(l2.v, l2.v, AF.Exp)
    nlam = cx.sbuf("nlam", [128, 1], F32)
    cx.tt(nlam.v, l2[:, 1:2], l2[:, 0:1], ALU.subtract)
    cx.ts(nlam.v, nlam.v, -LAMBDA_INIT_1, None, ALU.add)
    gs2 = cx.sbuf("gs2", [128, 1], F32)
    cx.ts(gs2.v, gs_sb.v, 1.0 - LAMBDA_INIT_1, None, ALU.mult)
    QT = cx.sbuf("QT", [128, ns], BF16)
    KT = cx.sbuf("KT", [128, ns], BF16)
    V = cx.sbuf("V", [128, nkt, 128], BF16)
    QTt = [cx.sub(QT[:, t * TT:(t + 1) * TT], f"QT{t}") for t in range(ntile)]
    KTt = [cx.sub(KT[:, t * TT:(t + 1) * TT], f"KT{t}") for t in range(ntile)]
    Vt = [cx.sub(V[:, t * 4:(t + 1) * 4, :], f"V{t}") for t in range(ntile)]
    hts = [cx.sbuf(f"ht{i}", [128, NCH, TT], BF16).split() for i in range(2)]
    wsb = {n: [cx.sbuf(f"{n}_sb{i}", [128, NCH * 128], BF16) for i in range(2)] for n in wts}
    cs = [cx.sbuf(f"cos{i}", [128, TT], F32) for i in range(2)]
    sn = [cx.sbuf(f"sin{i}", [128, TT], F32) for i in range(2)]
    tmp = [cx.sbuf(f"rtmp{i}", [128, TT], F32) for i in range(4)]
    P = [cx.sbuf(f"P{i}", [128, TT], BF16) for i in range(4)]
    rz = [cx.sbuf(f"rz{i}", [128, TT], F32) for i in range(2)]
    o1 = cx.sbuf("o1", [128, TT], F32)
    o2 = cx.sbuf("o2", [128, TT], F32)
    osq = cx.sbuf("osq", [128, TT], BF16)
    ors = cx.sbuf("ors", [128, TT], F32)
    ost = [cx.sbuf(f"ost{i}", [128, TT], mixo.ap.dtype) for i in range(2)]

    def load_w(h):
        for n in wts:
            cx.dma(cx.pool, wsb[n][h % 2].v, wts[n][h])
    load_w(0)
    for h in range(nh):
        if h + 1 < nh:
            load_w(h + 1)
        W = {n: wsb[n][h % 2] for n in wts}
        for t in range(ntile):
            ht = hts[t % 2]
            cx.dma(cx.sp, ht.v, dram_tile(hT, t * TT, TT))
            cx.dma(cx.sp, cs[t % 2].v, cosT[:, t * TT:(t + 1) * TT])
            cx.dma(cx.sp, sn[t % 2].v, sinT[:, t * TT:(t + 1) * TT])
            emit_rope_proj(cx, cm, ht, W["wq"], W["wqs"], cs[t % 2], sn[t % 2], QTt[t].v, (B[0], B[1]), tmp[0:2])
            emit_rope_proj(cx, cm, ht, W["wk"], W["wks"], cs[t % 2], sn[t % 2], KTt[t].v, (B[2], B[3]), tmp[2:4])
            for sub in range(4):
                for c in range(NCH):
                    cx.mm(B[4][:, sub * 128:(sub + 1) * 128], ht.p[c][:, sub * 128:(sub + 1) * 128],
                          W["wv"][:, c * 128:(c + 1) * 128], start=(c == 0), stop=(c == NCH - 1))
            cx.activation(View(Vt[t], Vt[t].ap.rearrange("p a d -> p (a d)")), B[4].v, AF.Copy)
        for qt in range(ntile):
            nk = 4 * (qt + 1)
            qv = [QTt[qt][0:64, :], QTt[qt][64:128, :]]
            O = [B[4], B[5]]
            Z = [B[6], B[7]]

            def emit_S(kt):
                kb = KTt[kt // 4]
                ko = (kt % 4) * 128
                diag = kt >= 4 * qt
                for m in range(2):
                    sb = B[(kt % 2) * 2 + m]
                    kv = kb[m * 64:(m + 1) * 64, ko:ko + 128]
                    cx.mm(sb.v, kv, qv[m], start=True, stop=not diag)
                    if diag:
                        j = kt - 4 * qt
                        cx.mm(sb.v, id_sb.v, tri_sb[:, j * TT:(j + 1) * TT], start=False, stop=True)
            emit_S(0)
            for kt in range(nk):
                if kt + 1 < nk:
                    emit_S(kt + 1)
                vt = Vt[kt // 4][:, kt % 4, :]
                for m in range(2):
                    sb = B[(kt % 2) * 2 + m]
                    p = P[(kt % 2) * 2 + m]
                    cx.activation(p.v, sb.v, AF.Exp, scale=scale)
                    cx.mm(O[m].v, vt, p.v, start=(kt == 0), stop=(kt == nk - 1))
                    cx.mm(Z[m].v, cm.ones_bf.v, p.v, start=(kt == 0), stop=(kt == nk - 1))
            cx.recip(rz[0].v, Z[0].v)
            cx.recip(rz[1].v, Z[1].v)
            cx.tt(o1.v, O[0].v, rz[0].v, ALU.mult)
            cx.tt(o2.v, O[1].v, rz[1].v, ALU.mult)
            cx.stt(o1.v, o2.v, nlam.v, o1.v, ALU.mult, ALU.add)
            cx.activation(osq.v, o1.v, AF.Square)
            cx.mm(B[0].v, cm.ones_bf.v, osq.v, start=True, stop=True)
            cx.activation(ors.v, B[0].v, AF.Sqrt, bias=cm.eps_t.v, scale=1.0 / 128)
            cx.recip(ors.v, ors.v)
            st = ost[qt % 2]
            cx.stt(st.v, o1.v, gs2.v, ors.v, ALU.mult, ALU.mult)
            cx.dma(cx.sp, mixo[h * 128:(h + 1) * 128, qt * TT:(qt + 1) * TT], st.v, clkbuf=st)


MB = 256
BIGF = 1.0e30


def e_rows_host(ns):
    nb = ns // MB
    e = np.zeros((32, ns), np.float32)
    for n in range(nb):
        e[n, n * MB:(n + 1) * MB] = 1.0
    return e


def shift_host():
    s = np.zeros((128, 64), np.float32)
    s[64 + np.arange(64), np.arange(64)] = 1.0
    return s


def build_mixer0(ns, nh, ng, act_dt=BF16):
    nc = bass.Bass("TRN2", target_bir_lowering=False)
    cx = Ctx(nc)
    cm = Common(cx, nbanks=7)
    npair = nh // 2
    d = {}
    d["hT"] = cx.dram("hT", [D_MODEL, ns], act_dt, kind="ExternalInput")
    for n in ("wq", "wqs", "wk", "wks", "wv"):
        d[n] = cx.dram(n, [npair, 128, NCH * 128], F32, kind="ExternalInput")
    d["wu"] = cx.dram("wu", [ng // 2, 128, NCH * 128], F32, kind="ExternalInput")
    d["wgv"] = cx.dram("wgv", [128, NCH * ng * 64], F32, kind="ExternalInput")
    d["cosT"] = cx.dram("cosT", [128, ns], F32, kind="ExternalInput")
    d["sinT"] = cx.dram("sinT", [128, ns], F32, kind="ExternalInput")
    d["tri"] = cx.dram("tri", [128, 4 * TT], F32, kind="ExternalInput")
    d["ident"] = cx.dram("ident", [128, 128], F32, kind="ExternalInput")
    d["erows"] = cx.dram("erows", [32, ns], F32, kind="ExternalInput")
    d["shift"] = cx.dram("shift", [128, 64], F32, kind="ExternalInput")
    d["lng"] = cx.dram("lng", [128, ng * 64], F32, kind="ExternalInput")
    d["lnb"] = cx.dram("lnb", [128, ng * 64], F32, kind="ExternalInput")
    d["wsT"] = cx.dram("wsT", [128, ng * 128], F32, kind="ExternalInput")
    d["c01"] = cx.dram("c01", [128, 128], F32, kind="ExternalInput")
    d["bT"] = cx.dram("bT", [128, (ng // 2) * TT], F32, kind="ExternalInput")
    mixo = cx.dram("mixo", [nh * 64 + ng * 64, ns], act_dt, kind="ExternalOutput")
    emit_mixer0(cx, cm, ns, nh, ng, d, mixo)
    cx.finish([mixo])
    return nc


def emit_mixer0(cx, cm, ns, nh, ng, d, mixo):
    nc = cx.nc
    B = cm.banks
    ntile = ns // TT
    nkt = ns // 128
    nblk = ns // MB
    npair = nh // 2
    scale = HD ** -0.5
    odt = mixo.ap.dtype
    trp = cx.psum("trp", [128, 2, TT], BF16)
    trA = trp[:, 0, :]
    trB = trp[:, 1, :]
    tri_sb = cx.sbuf("tri_sb", [128, 4 * TT], BF16)
    cx.dma(cx.pool, tri_sb.v, d["tri"].v)
    id_sb = cx.sbuf("id_sb", [128, 128], BF16)
    cx.dma(cx.pool, id_sb.v, d["ident"].v)
    shift_sb = cx.sbuf("shift_sb", [128, 64], F32)
    cx.dma(cx.sp, shift_sb.v, d["shift"].v)
    Qa = [cx.sbuf(f"Qa{i}", [128, ns], BF16) for i in range(2)]
    Ka = [cx.sbuf(f"Ka{i}", [128, ns], BF16) for i in range(2)]
    Va = [cx.sbuf(f"Va{i}", [128, nkt, 128], BF16) for i in range(2)]
    rowsQ = [(0, 96), (32, 128)]
    Qt = [[cx.sub(Qa[i][:, t * TT:(t + 1) * TT], f"Qa{i}_{t}") for t in range(ntile)] for i in range(2)]
    Kt = [[cx.sub(Ka[i][:, t * TT:(t + 1) * TT], f"Ka{i}_{t}") for t in range(ntile)] for i in range(2)]
    Vt = [[cx.sub(Va[i][:, t * 4:(t + 1) * 4, :], f"Va{i}_{t}") for t in range(ntile)] for i in range(2)]
    kms = cx.sbuf("kms", [128, 32], F32)
    cx.memset(kms.v, 0.0)
    hts = [cx.sbuf(f"ht{i}", [128, NCH, TT], BF16).split() for i in range(2)]
    wnames = ("wq", "wqs", "wk", "wks", "wv")
    wsb = {n: [cx.sbuf(f"{n}_sb{i}", [128, NCH * 128], BF16) for i in range(2)] for n in wnames}
    cs = [cx.sbuf(f"cos{i}", [128, TT], F32) for i in range(2)]
    sn = [cx.sbuf(f"sin{i}", [128, TT], F32) for i in range(2)]
    tmp = [cx.sbuf(f"rtmp{i}", [128, TT], F32) for i in range(2)]
    rf = [cx.sbuf(f"ropef{i}", [128, TT], F32) for i in range(2)]
    P = [cx.sbuf(f"P{i}", [128, TT], BF16) for i in range(4)]
    gsb = cx.sbuf("gsb", [128, 8, 32], F32)
    m8 = cx.sbuf("m8", [128, 8, 8], F32)
    lt = cx.sbuf("ltm", [128, 8, 32], F32)
    selA = cx.sbuf("selA", [128, 4, 96], BF16)
    selB = cx.sbuf("selB", [128, 4, 32], BF16)
    cx.memset(selA.v, 0.0)
    cx.memset(selB.v, 0.0)
    rzf = cx.sbuf("rzf", [128, TT], F32)
    cx.memset(rzf.v, 0.0)
    rzs = cx.sbuf("rzs", [64, TT], F32)
    ost = [cx.sbuf(f"ost{i}", [128, TT], odt) for i in range(2)]
    for i in range(2):
        for t in range(ntile):
            cx.memset(Vt[i][t][:, :, 64:128], 1.0, eng=cx.pool)

    def load_w(pi):
        for n in wnames:
            cx.dma(cx.pool, wsb[n][pi % 2].v, d[n][pi])

    def proj_rope(ht, w, ws, t, banks, out_f):
        b0, b1 = banks
        for c in range(NCH):
            cx.mm(b0.v, w[:, c * 128:(c + 1) * 128], ht.p[c].v, start=(c == 0), stop=(c == NCH - 1))
        for c in range(NCH):
            cx.mm(b1.v, ws[:, c * 128:(c + 1) * 128], ht.p[c].v, start=(c == 0), stop=(c == NCH - 1))
        cx.tt(tmp[0].v, b0.v, cs[t % 2].v, ALU.mult)
        cx.tt(tmp[1].v, b1.v, sn[t % 2].v, ALU.mult)
        cx.tt(out_f.v, tmp[0].v, tmp[1].v, ALU.add)

    def load_tile(t, with_rope=True):
        ht = hts[t % 2]
        cx.dma(cx.sp, ht.v, dram_tile(d["hT"], t * TT, TT))
        if with_rope:
            cx.dma(cx.sp, cs[t % 2].v, d["cosT"][:, t * TT:(t + 1) * TT])
            cx.dma(cx.sp, sn[t % 2].v, d["sinT"][:, t * TT:(t + 1) * TT])
        return ht

    load_w(0)
    for pi in range(npair):
        if pi + 1 < npair:
            load_w(pi + 1)
        W = {n: wsb[n][pi % 2] for n in wnames}
        for t in range(ntile):
            cx.dma(cx.pool, Kt[0][t][64:96, :], d["erows"][:, t * TT:(t + 1) * TT])
            cx.dma(cx.pool, Kt[1][t][0:32, :], d["erows"][:, t * TT:(t + 1) * TT])
        for t in range(ntile):
            ht = load_tile(t)
            r = rf[t % 2]
            proj_rope(ht, W["wk"], W["wks"], t, (B[0], B[1]), r)
            cx.copy(Kt[0][t][0:64, :], r[0:64, :])
            cx.copy(Kt[1][t][64:128, :], r[64:128, :], eng=cx.pool)
            cx.op(cx.dve, lambda: nc.vector.reduce_sum(out=kms.ap[:, 2 * t:2 * t + 2],
                                                       in_=r.ap.rearrange("p (a k) -> p a k", a=2), axis=AX.X),
                  [r.v], [kms.v])
            for sub in range(4):
                for c in range(NCH):
                    cx.mm(B[2][:, sub * 128:(sub + 1) * 128], ht.p[c][:, sub * 128:(sub + 1) * 128],
                          W["wv"][:, c * 128:(c + 1) * 128], start=(c == 0), stop=(c == NCH - 1))
            b2v = B[2].ap.rearrange("p (a x) -> p a x", a=4)
            cx.activation(Vt[0][t][:, :, 0:64], View(B[2], b2v[:, :, 0:64]), AF.Copy)
            cx.activation(Vt[1][t][:, :, 0:64], View(B[2], b2v[:, :, 64:128]), AF.Copy)
        for t in range(ntile):
            ht = load_tile(t)
            r = rf[t % 2]
            proj_rope(ht, W["wq"], W["wqs"], t, (B[0], B[1]), r)
            cx.copy(Qt[0][t][0:64, :], r[0:64, :])
            cx.copy(Qt[1][t][64:128, :], r[64:128, :], eng=cx.pool)
            gb = B[3]
            for sub in range(4):
                for hh in range(2):
                    k = sub * 2 + hh
                    cx.mm(gb[:, k * 32:(k + 1) * 32], r[hh * 64:(hh + 1) * 64, sub * 128:(sub + 1) * 128],
                          kms[hh * 64:(hh + 1) * 64, :], start=True, stop=True)
            cx.memset(gsb.v, NEG * 1e6)
            gbv = gb.ap[:, 0:256].rearrange("p (k n) -> p k n", k=8)
            for grp in range(2):
                n_own = 2 * t + grp
                if n_own > 0:
                    cx.copy(gsb[:, grp * 4:(grp + 1) * 4, 0:n_own], View(gb, gbv[:, grp * 4:(grp + 1) * 4, 0:n_own]))
            for k in range(8):
                cx.op(cx.dve, lambda k=k: nc.vector.max(out=m8.ap[:, k, :], in_=gsb.ap[:, k, :]), [gsb.v], [m8.v])
            for grp in range(2):
                n_own = 2 * t + grp
                cx.memset(gsb[:, grp * 4:(grp + 1) * 4, n_own:n_own + 1], BIGF)
            cx.tt(lt.v, gsb.v, View(m8, m8.ap[:, :, 2:3].to_broadcast([128, 8, 32])), ALU.is_lt)
            ltv = lt.ap.rearrange("p (s h) n -> p s h n", h=2)
            cx.ts(selA[:, :, 64:96], View(lt, ltv[:, :, 0, :]), NEG, None, ALU.mult)
            cx.ts(selB.v, View(lt, ltv[:, :, 1, :]), NEG, None, ALU.mult)
            for sub in range(4):
                cx.op(cx.pe, lambda sub=sub: nc.tensor.transpose(trA.ap[0:96, sub * 128:(sub + 1) * 128],
                                                                 selA.ap[:, sub, :], id_sb.ap), [selA.v, id_sb.v], [trA])
                cx.op(cx.pe, lambda sub=sub: nc.tensor.transpose(trB.ap[0:32, sub * 128:(sub + 1) * 128],
                                                                 selB.ap[:, sub, :], id_sb.ap), [selB.v, id_sb.v], [trB])
            cx.activation(Qt[0][t][64:96, :], trA[64:96, :], AF.Copy)
            cx.activation(Qt[1][t][0:32, :], trB[0:32, :], AF.Copy)
        for hh in range(2):
            r0, r1 = rowsQ[hh]
            for qt in range(ntile):
                nk = 4 * (qt + 1)
                OZ = B[4 + (qt % 2)]

                def emit_S(kt):
                    sb = B[kt % 2]
                    kc = slice((kt % 4) * 128, (kt % 4 + 1) * 128)
                    diag = kt >= 4 * qt
                    if hh == 0:
                        cx.mm(sb.v, Kt[0][kt // 4][0:96, kc], Qt[0][qt][0:96, :], start=True, stop=not diag)
                    else:
                        cx.mm(sb.v, Kt[1][kt // 4][64:128, kc], Qt[1][qt][64:128, :], start=True, stop=False)
                        cx.mm(sb.v, Kt[1][kt // 4][0:32, kc], Qt[1][qt][0:32, :], start=False, stop=not diag)
                    if diag:
                        j = kt - 4 * qt
                        cx.mm(sb.v, id_sb.v, tri_sb[:, j * TT:(j + 1) * TT], start=False, stop=True)
                emit_S(0)
                for kt in range(nk):
                    if kt + 1 < nk:
                        emit_S(kt + 1)
                    p = P[kt % 4]
                    cx.activation(p.v, B[kt % 2].v, AF.Exp, scale=scale)
                    cx.mm(OZ.v, Vt[hh][kt // 4][:, kt % 4, :], p.v, start=(kt == 0), stop=(kt == nk - 1))
                cx.op(cx.dve, lambda: nc.vector.reciprocal(out=rzf.ap[64:128, :], in_=OZ.ap[64:128, :]), [OZ.v], [rzf.v])
                cx.mm(B[6][0:64, :], shift_sb.v, rzf.v, start=True, stop=True)
                cx.activation(rzs.v, B[6][0:64, :], AF.Copy)
                st = ost[qt % 2]
                cx.tt(st[0:64, :], OZ[0:64, :], rzs.v, ALU.mult)
                head = pi * 2 + hh
                cx.dma(cx.sp, mixo[head * 64:(head + 1) * 64, qt * TT:(qt + 1) * TT], st[0:64, :], clkbuf=st)
    ngp = ng // 2
    GW = ng * 64
    wu_sb = [cx.sbuf(f"wu_sb{i}", [128, NCH * 128], BF16) for i in range(ngp)]
    for gp in range(ngp):
        cx.dma(cx.pool, wu_sb[gp].v, d["wu"][gp])
    wgv_sb = cx.sbuf("wgv_sb", [128, NCH * GW], BF16)
    cx.dma(cx.pool, wgv_sb.v, d["wgv"].v)
    lng = cx.sbuf("lng_sb", [128, GW], F32)
    lnb = cx.sbuf("lnb_sb", [128, GW], F32)
    cx.dma(cx.sp, lng.v, d["lng"].v)
    cx.dma(cx.sp, lnb.v, d["lnb"].v)
    wsf = cx.sbuf("wsf", [128, ng * 128], F32)
    c01 = cx.sbuf("c01_sb", [128, 128], F32)
    cx.dma(cx.sp, wsf.v, d["wsT"].v)
    cx.dma(cx.sp, c01.v, d["c01"].v)
    wsm = cx.sbuf("wsm", [128, ng, 128], BF16)
    cx.tt(wsm.v, View(wsf, wsf.ap.rearrange("p (g t) -> p g t", g=ng)),
          View(c01, c01.ap.unsqueeze(1).to_broadcast([128, ng, 128])), ALU.mult)
    bT = cx.sbuf("bT_sb", [128, ngp * TT], F32)
    cx.dma(cx.sp, bT.v, d["bT"].v)
    u_sb = [cx.sbuf(f"u_sb{i}", [128, TT], F32) for i in range(ngp)]
    v_sb = cx.sbuf("v_sb", [128, GW], F32)
    vsq = cx.sbuf("vsq", [128, GW], F32)
    st1 = cx.sbuf("st1", [128, ng], F32)
    st2 = cx.sbuf("st2", [128, ng], F32)
    mean = cx.sbuf("mean", [128, ng], F32)
    msq = cx.sbuf("msq", [128, ng], F32)
    var = cx.sbuf("var", [128, ng], F32)
    vc = cx.sbuf("vc", [128, GW], F32)
    vnp = [cx.sbuf(f"vnp{i}", [128, ng, 128], BF16) for i in range(2)]
    for i in range(2):
        cx.memset(vnp[i].v, 0.0)
    gt = cx.sbuf("gtmp", [128, TT], F32)

    def g3(buf):
        return buf.ap.rearrange("p (g d) -> p g d", g=ng)

    def bc(buf):
        return View(buf, buf.ap.unsqueeze(2).to_broadcast([128, ng, 64]))

    for t in range(ntile):
        ht = load_tile(t, with_rope=False)
        for gp in range(ngp):
            for c in range(NCH):
                cx.mm(B[gp].v, wu_sb[gp][:, c * 128:(c + 1) * 128], ht.p[c].v, start=(c == 0), stop=(c == NCH - 1))
            cx.activation(u_sb[gp].v, B[gp].v, AF.Gelu_apprx_tanh)
        for sub in range(4):
            vb = B[2 + (sub % 2)]
            for c in range(NCH):
                cx.mm(vb[:, 0:GW], ht.p[c][:, sub * 128:(sub + 1) * 128], wgv_sb[:, c * GW:(c + 1) * GW],
                      start=(c == 0), stop=(c == NCH - 1))
            cx.activation(v_sb.v, vb[:, 0:GW], AF.Gelu_apprx_tanh)
            cx.activation(vsq.v, v_sb.v, AF.Square)
            cx.op(cx.dve, lambda: nc.vector.reduce_sum(out=st1.ap, in_=g3(v_sb), axis=AX.X), [v_sb.v], [st1.v])
            cx.op(cx.dve, lambda: nc.vector.reduce_sum(out=st2.ap, in_=g3(vsq), axis=AX.X), [vsq.v], [st2.v])
            cx.ts(mean.v, st1.v, 1.0 / 64, None, ALU.mult)
            cx.tt(msq.v, mean.v, mean.v, ALU.mult)
            cx.stt(var.v, st2.v, 1.0 / 64, msq.v, ALU.mult, ALU.subtract)
            cx.activation(var.v, var.v, AF.Sqrt, bias=cm.eps_t.v, scale=1.0)
            cx.recip(var.v, var.v)
            cx.tt(View(vc, g3(vc)), View(v_sb, g3(v_sb)), bc(mean), ALU.subtract)
            cx.tt(View(vc, g3(vc)), View(vc, g3(vc)), bc(var), ALU.mult)
            cx.tt(vc.v, vc.v, lng.v, ALU.mult)
            vp = vnp[sub % 2]
            for par in range(2):
                src = g3(vc).rearrange("p (a b) d -> p a b d", b=2)[:, :, par, :]
                lb = g3(lnb).rearrange("p (a b) d -> p a b d", b=2)[:, :, par, :]
                dst = vp.ap.rearrange("p (a b) x -> p a b x", b=2)[:, :, par, par * 64:(par + 1) * 64]
                cx.tt(View(vp, dst), View(vc, src), View(lnb, lb), ALU.add)
            for gp in range(ngp):
                mb = B[4 + gp]
                for par in range(2):
                    g = gp * 2 + par
                    cx.mm(mb[:, sub * 128:(sub + 1) * 128], vp[:, g, :], wsm[:, g, :], start=(par == 0), stop=(par == 1),
                          signal=True)
        for gp in range(ngp):
            cx.tt(gt.v, B[4 + gp].v, bT[:, gp * TT:(gp + 1) * TT], ALU.add)
            st = ost[gp % 2]
            cx.tt(st.v, gt.v, u_sb[gp].v, ALU.mult)
            row0 = nh * 64 + gp * 128
            cx.dma(cx.sp, mixo[row0:row0 + 128, t * TT:(t + 1) * TT], st.v, clkbuf=st)


def host_mixer0_inputs(ns, nh, ng, wq, wk, wv, wu, wgv, ln_g, ln_b, w_s, b_s):
    ct, st = rope_tables_host(ns)
    GW = ng * 64
    im = {"wq": wtile(wq), "wqs": wtile(swap_cols(wq)), "wk": wtile(wk), "wks": wtile(swap_cols(wk)),
          "wv": wtile(wv), "wu": wtile(wu),
          "wgv": np.ascontiguousarray(wgv.reshape(NCH, 128, GW).transpose(1, 0, 2).reshape(128, NCH * GW)),
          "cosT": ct, "sinT": st, "tri": tri_host(), "ident": np.eye(128, dtype=np.float32),
          "erows": e_rows_host(ns), "shift": shift_host(),
          "lng": np.ascontiguousarray(np.broadcast_to(ln_g.reshape(1, GW), (128, GW))),
          "lnb": np.ascontiguousarray(np.broadcast_to(ln_b.reshape(1, GW), (128, GW))),
          "wsT": np.ascontiguousarray(w_s.transpose(2, 0, 1).reshape(128, ng * 128)),
          "c01": np.triu(np.ones((128, 128), np.float32)),
          }
    bT = np.zeros((128, (ng // 2) * TT), np.float32)
    for gp in range(ng // 2):
        for par in range(2):
            row = np.tile(b_s[gp * 2 + par], TT // 128)
            bT[par * 64:(par + 1) * 64, gp * TT:(gp + 1) * TT] = row[None, :]
    im["bT"] = bT
    return im


NCORES = 8
HALF = SEQ // 2
_PROGS = {}


def _prog(key, fn):
    if key not in _PROGS:
        _PROGS[key] = fn()
    return _PROGS[key]


def gains_layout(gs):
    return np.ascontiguousarray(np.concatenate([np.asarray(g, np.float32).reshape(NCH, 128).T for g in gs], axis=1))


def wout_layout(w):
    return np.ascontiguousarray(np.asarray(w, np.float32).reshape(NCH, 128, D_MODEL).transpose(1, 0, 2)
                                .reshape(128, NCH * D_MODEL))


def _run(nc, in_maps):
    res = run_bass_kernel_spmd(nc, in_maps, core_ids=list(range(NCORES)))
    return res.results


def kernel(x, ffn_pre_norm, ffn_pre_w_gate, ffn_pre_w_up, ffn_pre_w_down, mix_norm,
           ffn_post_norm, ffn_post_w_gate, ffn_post_w_up, ffn_post_w_down,
           even_w_in, even_w_out, gmlp_ln_g, gmlp_ln_b, gmlp_w_s, gmlp_b_s,
           odd_w_in, odd_w_out, diff_lambda_q1, diff_lambda_k1, diff_lambda_q2,
           diff_lambda_k2, diff_subln_g, final_norm):
    f = lambda a: np.asarray(a, np.float32)
    x = f(x)
    ADT = F32
    xT = [np.ascontiguousarray(x[c // 2, (c % 2) * HALF:(c % 2 + 1) * HALF, :].T) for c in range(NCORES)]

    def ffn_w(name, wg, wu, wd, layer):
        return host_ffn_weights(name, f(wg[layer]), f(wu[layer]), f(wd[layer]))

    nc = _prog("rl1", lambda: build_rowlocal(HALF, 1, False, False, act_dt=ADT))
    shared = {"gains": gains_layout([ffn_pre_norm[0], mix_norm[0]])}
    shared.update(ffn_w("ffn0", ffn_pre_w_gate, ffn_pre_w_up, ffn_pre_w_down, 0))
    r = _run(nc, [dict(shared, xT=xT[c]) for c in range(NCORES)])
    xT = [r[c]["xTo"] for c in range(NCORES)]
    hT = [r[c]["hTo"] for c in range(NCORES)]

    nc = _prog("m0", lambda: build_mixer0(SEQ, 4, 4, act_dt=ADT))
    w_in = f(even_w_in[0])
    gz0 = 3 * 512
    per_s = []
    for s in range(2):
        cs_ = slice(256 * s, 256 * s + 256)
        im = host_mixer0_inputs(SEQ, 4, 4, w_in[:, 0:512][:, cs_], w_in[:, 512:1024][:, cs_], w_in[:, 1024:1536][:, cs_],
                                w_in[:, gz0:gz0 + 512][:, cs_], w_in[:, gz0 + 512:gz0 + 1024][:, cs_],
                                f(gmlp_ln_g[0])[4 * s:4 * s + 4], f(gmlp_ln_b[0])[4 * s:4 * s + 4],
                                f(gmlp_w_s[0])[4 * s:4 * s + 4], f(gmlp_b_s[0])[4 * s:4 * s + 4])
        per_s.append(im)
    hfull = [np.ascontiguousarray(np.concatenate([hT[2 * b], hT[2 * b + 1]], axis=1)) for b in range(BATCH)]
    r = _run(nc, [dict(per_s[c % 2], hT=hfull[c // 2]) for c in range(NCORES)])
    mix = []
    for c in range(NCORES):
        b, s = c // 2, c % 2
        ts_ = slice(s * HALF, (s + 1) * HALF)
        m0, m1 = r[2 * b]["mixo"], r[2 * b + 1]["mixo"]
        mix.append(np.ascontiguousarray(np.concatenate([m0[0:256, ts_], m1[0:256, ts_], m0[256:512, ts_], m1[256:512, ts_]], axis=0)))

    nc = _prog("rl3", lambda: build_rowlocal(HALF, 2, True, False, act_dt=ADT))
    shared = {"gains": gains_layout([ffn_post_norm[0], ffn_pre_norm[1], mix_norm[1]]), "wout": wout_layout(even_w_out[0])}
    shared.update(ffn_w("ffn0", ffn_post_w_gate, ffn_post_w_up, ffn_post_w_down, 0))
    shared.update(ffn_w("ffn1", ffn_pre_w_gate, ffn_pre_w_up, ffn_pre_w_down, 1))
    r = _run(nc, [dict(shared, xT=xT[c], mixT=mix[c]) for c in range(NCORES)])
    xT = [r[c]["xTo"] for c in range(NCORES)]
    hT = [r[c]["hTo"] for c in range(NCORES)]

    nc = _prog("m1", lambda: build_mixer1(SEQ, 4, act_dt=ADT))
    w_in = f(odd_w_in[0])
    ct, st = rope_tables_host(SEQ)
    lam = np.ascontiguousarray(np.broadcast_to(np.concatenate([f(diff_lambda_q1[0]), f(diff_lambda_k1[0]),
                                                               f(diff_lambda_q2[0]), f(diff_lambda_k2[0])])[None, :], (128, 4 * HD)))
    per_s = []
    for s in range(2):
        cs_ = slice(512 * s, 512 * s + 512)
        wq, wk, wv = w_in[:, 0:1024][:, cs_], w_in[:, 1024:2048][:, cs_], w_in[:, 2048:3072][:, cs_]
        per_s.append({"wq": wtile(wq), "wqs": wtile(swap_cols(wq)), "wk": wtile(wk), "wks": wtile(swap_cols(wk)),
                      "wv": wtile(wv), "cosT": ct, "sinT": st, "tri": tri_host(), "ident": np.eye(128, dtype=np.float32),
                      "lam": lam, "gsub": f(diff_subln_g[0]).reshape(128, 1).copy()})
    hfull = [np.ascontiguousarray(np.concatenate([hT[2 * b], hT[2 * b + 1]], axis=1)) for b in range(BATCH)]
    r = _run(nc, [dict(per_s[c % 2], hT=hfull[c // 2]) for c in range(NCORES)])
    mix = []
    for c in range(NCORES):
        b, s = c // 2, c % 2
        ts_ = slice(s * HALF, (s + 1) * HALF)
        mix.append(np.ascontiguousarray(np.concatenate([r[2 * b]["mixo"][:, ts_], r[2 * b + 1]["mixo"][:, ts_]], axis=0)))

    nc = _prog("rl5", lambda: build_rowlocal(HALF, 1, True, True, act_dt=ADT))
    shared = {"gains": gains_layout([ffn_post_norm[1], final_norm]), "wout": wout_layout(odd_w_out[0])}
    shared.update(ffn_w("ffn0", ffn_post_w_gate, ffn_post_w_up, ffn_post_w_down, 1))
    r = _run(nc, [dict(shared, xT=xT[c], mixT=mix[c]) for c in range(NCORES)])
    out = np.empty((BATCH, SEQ, D_MODEL), np.float32)
    for c in range(NCORES):
        out[c // 2, (c % 2) * HALF:(c % 2 + 1) * HALF, :] = r[c]["outT"].T
    return out
```
